# Optimizing a Trainium2 kernel written in Bass

```python
import math
import jax, jax.numpy as jnp
from jax import lax
import numpy as np

D_MODEL = 1024
BATCH = 4
SEQ = 4096
DEPTH = 2

GRID_W = 64
CTX_LEN = 256
N_MIXERS = 2
N_ATTN_LAYERS = (DEPTH + 1) // 2
N_SSM_LAYERS = DEPTH // 2
EPS = 1e-6

HEAD_DIM = 64
N_HEADS = D_MODEL // HEAD_DIM
N_KV_HEADS = N_HEADS // 4
Q_PER_KV = N_HEADS // N_KV_HEADS
Q_DIM = N_HEADS * HEAD_DIM
QKV_DIM = (N_HEADS + 2 * N_KV_HEADS) * HEAD_DIM
WINDOW = 128
BLOCK = 128
ROPE_FREQS = HEAD_DIM // 4
ROPE_BASE = 10000.0

D_INNER = 2 * D_MODEL
SSM_HEAD_DIM = 64
SSM_HEADS = D_INNER // SSM_HEAD_DIM
SSM_GROUPS = 8
HEADS_PER_GROUP = SSM_HEADS // SSM_GROUPS
D_STATE = 128
D_CONV = 3
CHUNK = 128
CONV_DIM = D_INNER + 2 * SSM_GROUPS * D_STATE
IN_PROJ_DIM = D_INNER + CONV_DIM + 2 * SSM_HEADS

D_FF = ((8 * D_MODEL // 3 + 255) // 256) * 256

kernel_name = "hybrid_swa_ssd_diffusion_trunk"

F32 = jnp.float32


def rmsnorm(x, g):
    x32 = x.astype(F32)
    y = x32 * lax.rsqrt(jnp.mean(x32 * x32, axis=-1, keepdims=True) + EPS)
    return (y * g.astype(F32)).astype(x.dtype)


def modulate(x, g, shift, scale):
    return rmsnorm(x, g) * (1 + scale) + shift


def swiglu(h, w_gate, w_up, w_down):
    return (jax.nn.silu(h @ w_gate) * (h @ w_up)) @ w_down


def axial_rope_tables(L):
    rows = L // GRID_W
    row = jnp.repeat(jnp.arange(rows, dtype=F32), GRID_W)
    col = jnp.tile(jnp.arange(GRID_W, dtype=F32), rows)
    inv = ROPE_BASE ** (-jnp.arange(ROPE_FREQS, dtype=F32) / ROPE_FREQS)
    ang = jnp.stack([row, col], axis=-1)[:, :, None] * inv
    return jnp.cos(ang), jnp.sin(ang)


def axial_rope(x, cos, sin):
    xs = x.reshape(x.shape[:-1] + (2, 2, ROPE_FREQS))
    x1, x2 = xs[..., 0, :], xs[..., 1, :]
    c, s = cos[None, :, None], sin[None, :, None]
    out = jnp.stack([x1 * c - x2 * s, x2 * c + x1 * s], axis=-2)
    return out.reshape(x.shape).astype(x.dtype)


def softmax_with_sink(logits, sink):
    full = jnp.concatenate([logits, jnp.broadcast_to(sink, logits.shape[:-1] + (1,))], axis=-1)
    return jax.nn.softmax(full, axis=-1)[..., :-1]


def attention_mixer(h, hc, w_qkv, w_o, sinks, cos, sin, with_ctx_out):
    b, L, _ = h.shape
    C = hc.shape[1]
    q, k, v = jnp.split(h @ w_qkv, [Q_DIM, Q_DIM + N_KV_HEADS * HEAD_DIM], axis=-1)
    q = axial_rope(q.reshape(b, L, N_HEADS, HEAD_DIM), cos, sin)
    k = axial_rope(k.reshape(b, L, N_KV_HEADS, HEAD_DIM), cos, sin)
    v = v.reshape(b, L, N_KV_HEADS, HEAD_DIM)
    q = q.reshape(b, L, N_KV_HEADS, Q_PER_KV, HEAD_DIM)
    kc, vc = jnp.split(hc @ w_qkv[:, Q_DIM:], 2, axis=-1)
    kc = kc.reshape(b, C, N_KV_HEADS, HEAD_DIM)
    vc = vc.reshape(b, C, N_KV_HEADS, HEAD_DIM)
    sink = sinks.astype(F32).reshape(1, N_KV_HEADS, Q_PER_KV, 1, 1)
    scale = HEAD_DIM ** -0.5
    pad = ((0, 0), (BLOCK, BLOCK), (0, 0), (0, 0))
    kp, vp = jnp.pad(k, pad), jnp.pad(v, pad)
    offs_q = jnp.arange(BLOCK)
    offs_k = jnp.arange(3 * BLOCK) - BLOCK

    def block(n):
        start = n * BLOCK
        qb = lax.dynamic_slice_in_dim(q, start, BLOCK, axis=1)
        kb = lax.dynamic_slice_in_dim(kp, start, 3 * BLOCK, axis=1)
        vb = lax.dynamic_slice_in_dim(vp, start, 3 * BLOCK, axis=1)
        q_pos = start + offs_q
        k_pos = start + offs_k
        valid = ((jnp.abs(k_pos[None, :] - q_pos[:, None]) <= WINDOW)
                 & (k_pos >= 0)[None, :] & (k_pos < L)[None, :])
        s_ctx = jnp.einsum('bqgrd,bkgd->bgrqk', qb, kc).astype(F32) * scale
        s_win = jnp.einsum('bqgrd,bkgd->bgrqk', qb, kb).astype(F32) * scale
        s_win = jnp.where(valid, s_win, -jnp.inf)
        p = softmax_with_sink(jnp.concatenate([s_ctx, s_win], axis=-1), sink).astype(v.dtype)
        return (jnp.einsum('bgrqk,bkgd->bqgrd', p[..., :C], vc)
                + jnp.einsum('bgrqk,bkgd->bqgrd', p[..., C:], vb))

    o = lax.map(block, jnp.arange(L // BLOCK))
    o = jnp.moveaxis(o, 0, 1).reshape(b, L, Q_DIM)
    y = o @ w_o
    if not with_ctx_out:
        return y, None
    qc = (hc @ w_qkv[:, :Q_DIM]).reshape(b, C, N_KV_HEADS, Q_PER_KV, HEAD_DIM)
    sc = jnp.einsum('bqgrd,bkgd->bgrqk', qc, kc).astype(F32) * scale
    pc = softmax_with_sink(sc, sink).astype(vc.dtype)
    oc = jnp.einsum('bgrqk,bkgd->bqgrd', pc, vc).reshape(b, C, Q_DIM)
    return y, oc @ w_o


def centred_depthwise_conv(u, w, bias):
    K = w.shape[0]
    out = lax.conv_general_dilated(u, w[:, None, :], window_strides=(1,),
                                   padding=[(K // 2, K // 2)],
                                   dimension_numbers=('NWC', 'WIO', 'NWC'),
                                   feature_group_count=u.shape[-1])
    return out + bias


def ssm_project(u, w_in, conv_w, conv_b, dt_bias):
    b, L, _ = u.shape
    z, xbc, dt = jnp.split(u @ w_in, [D_INNER, D_INNER + CONV_DIM], axis=-1)
    xbc = jax.nn.silu(centred_depthwise_conv(xbc, conv_w, conv_b))
    xs, Bm, Cm = jnp.split(xbc, [D_INNER, D_INNER + SSM_GROUPS * D_STATE], axis=-1)
    dt = jax.nn.softplus(dt.reshape(b, L, 2, SSM_HEADS).astype(F32) + dt_bias.astype(F32))
    return (z,
            xs.reshape(b, L, SSM_GROUPS, HEADS_PER_GROUP, SSM_HEAD_DIM),
            Bm.reshape(b, L, SSM_GROUPS, D_STATE),
            Cm.reshape(b, L, SSM_GROUPS, D_STATE),
            dt.reshape(b, L, 2, SSM_GROUPS, HEADS_PER_GROUP))


def ssd_chunked(xs, dt, A, Bm, Cm, h0):
    b, L, G, HG, P = xs.shape
    nc = L // CHUNK
    x = xs.astype(F32).reshape(b, nc, CHUNK, G, HG, P)
    dt = dt.reshape(b, nc, CHUNK, G, HG)
    Bc = Bm.astype(F32).reshape(b, nc, CHUNK, G, D_STATE)
    Cc = Cm.astype(F32).reshape(b, nc, CHUNK, G, D_STATE)
    cum = jnp.cumsum(jnp.moveaxis(dt * A, 2, -1), axis=-1)
    xdt = x * dt[..., None]
    tri = jnp.tril(jnp.ones((CHUNK, CHUNK), dtype=bool))
    decay = jnp.exp(jnp.where(tri, cum[..., :, None] - cum[..., None, :], -jnp.inf))
    cb = jnp.einsum('bcign,bcjgn->bcgij', Cc, Bc)
    y_diag = jnp.einsum('bcghij,bcjghp->bcighp', cb[:, :, :, None] * decay, xdt)
    to_end = jnp.exp(cum[..., -1:] - cum)
    states = jnp.einsum('bcghj,bcjgn,bcjghp->bcghpn', to_end, Bc, xdt)
    chunk_decay = jnp.exp(cum[..., -1])

    def step(hs, inp):
        s, d = inp
        return d[..., None, None] * hs + s, hs

    h_final, h_start = lax.scan(step, h0.astype(F32),
                                (jnp.moveaxis(states, 1, 0), jnp.moveaxis(chunk_decay, 1, 0)))
    h_start = jnp.moveaxis(h_start, 0, 1)
    y_off = jnp.einsum('bcign,bcghpn,bcghi->bcighp', Cc, h_start, jnp.exp(cum))
    return (y_diag + y_off).reshape(b, L, G, HG, P).astype(xs.dtype), h_final


def ssd_final_state(xs, dt, A, Bm):
    cum = jnp.cumsum(dt * A, axis=1)
    decay = jnp.exp(cum[:, -1:] - cum)
    return jnp.einsum('blgh,blgn,blghp->bghpn', decay * dt, Bm.astype(F32), xs.astype(F32))


def bidirectional_ssd(xs, dt, A, Bm, Cm, h0_fwd, h0_bwd):
    flip = lambda u: jnp.flip(u, axis=1)
    y_f, h_f = ssd_chunked(xs, dt[:, :, 0], A[0], Bm, Cm, h0_fwd)
    y_b, h_b = ssd_chunked(flip(xs), flip(dt[:, :, 1]), A[1], flip(Bm), flip(Cm), h0_bwd)
    return y_f + flip(y_b), h_f, h_b


def ssm_output(y, xs, z, D_skip, norm_g, w_out):
    b, L = y.shape[:2]
    y = y + D_skip.reshape(SSM_GROUPS, HEADS_PER_GROUP)[:, :, None] * xs
    y = y.reshape(b, L, SSM_GROUPS, D_INNER // SSM_GROUPS) * jax.nn.silu(z).reshape(b, L, SSM_GROUPS, -1)
    y = rmsnorm(y, norm_g.reshape(SSM_GROUPS, -1))
    return y.reshape(b, L, D_INNER) @ w_out


def ssm_mixer(h, hc, w_in, conv_w, conv_b, dt_bias, A_log, D_skip, norm_g, w_out, with_ctx_out):
    A = -jnp.exp(A_log.astype(F32)).reshape(2, SSM_GROUPS, HEADS_PER_GROUP)
    zc, xc, Bc, Cc, dtc = ssm_project(hc, w_in, conv_w, conv_b, dt_bias)
    if with_ctx_out:
        zeros = jnp.zeros((hc.shape[0], SSM_GROUPS, HEADS_PER_GROUP, SSM_HEAD_DIM, D_STATE), F32)
        yc, hc_f, hc_b = bidirectional_ssd(xc, dtc, A, Bc, Cc, zeros, zeros)
        out_c = ssm_output(yc, xc, zc, D_skip, norm_g, w_out)
    else:
        hc_f = ssd_final_state(xc, dtc[:, :, 0], A[0], Bc)
        hc_b = ssd_final_state(jnp.flip(xc, 1), jnp.flip(dtc[:, :, 1], 1), A[1], jnp.flip(Bc, 1))
        out_c = None
    z, xs, Bm, Cm, dt = ssm_project(h, w_in, conv_w, conv_b, dt_bias)
    y, _, _ = bidirectional_ssd(xs, dt, A, Bm, Cm, hc_f, hc_b)
    return ssm_output(y, xs, z, D_skip, norm_g, w_out), out_c


def setup_inputs(seed: int = 0) -> dict:
    key = jax.random.key(seed)
    ks = jax.random.split(key, 24)
    nrm = lambda k, shape, s: jax.random.normal(k, shape, F32) * s
    dt_init = jnp.exp(jax.random.uniform(ks[16], (N_SSM_LAYERS, 2, SSM_HEADS), F32,
                                         minval=math.log(1e-3), maxval=math.log(1e-1)))
    return {
        "x": nrm(ks[0], (BATCH, SEQ, D_MODEL), 1.0),
        "c": nrm(ks[1], (BATCH, D_MODEL), 1.0),
        "ctx": nrm(ks[2], (BATCH, CTX_LEN, D_MODEL), 1.0),
        "c_ctx": nrm(ks[3], (D_MODEL,), 1.0),
        "ada_w": nrm(ks[4], (DEPTH, D_MODEL, 6 * D_MODEL), 0.5 * D_MODEL ** -0.5),
        "ada_b": nrm(ks[5], (DEPTH, 6 * D_MODEL), 0.01),
        "norm_mix_g": 1.0 + nrm(ks[6], (DEPTH, D_MODEL), 0.05),
        "norm_ffn_g": 1.0 + nrm(ks[7], (DEPTH, D_MODEL), 0.05),
        "attn_w_qkv": nrm(ks[8], (N_ATTN_LAYERS, D_MODEL, QKV_DIM), D_MODEL ** -0.5),
        "attn_w_o": nrm(ks[9], (N_ATTN_LAYERS, Q_DIM, D_MODEL), Q_DIM ** -0.5),
        "attn_sinks": nrm(ks[10], (N_ATTN_LAYERS, N_HEADS), 1.0),
        "ssm_w_in": nrm(ks[11], (N_SSM_LAYERS, D_MODEL, IN_PROJ_DIM), D_MODEL ** -0.5),
        "ssm_conv_w": nrm(ks[12], (N_SSM_LAYERS, D_CONV, CONV_DIM), D_CONV ** -0.5),
        "ssm_conv_b": nrm(ks[13], (N_SSM_LAYERS, CONV_DIM), 0.01),
        "ssm_dt_bias": dt_init + jnp.log(-jnp.expm1(-dt_init)),
        "ssm_A_log": jnp.log(jax.random.uniform(ks[14], (N_SSM_LAYERS, 2, SSM_HEADS), F32, minval=1.0, maxval=16.0)),
        "ssm_D": 1.0 + nrm(ks[15], (N_SSM_LAYERS, SSM_HEADS), 0.1),
        "ssm_norm_g": 1.0 + nrm(ks[17], (N_SSM_LAYERS, D_INNER), 0.05),
        "ssm_w_out": nrm(ks[18], (N_SSM_LAYERS, D_INNER, D_MODEL), D_INNER ** -0.5),
        "ffn_w_gate": nrm(ks[19], (DEPTH, D_MODEL, D_FF), D_MODEL ** -0.5),
        "ffn_w_up": nrm(ks[20], (DEPTH, D_MODEL, D_FF), D_MODEL ** -0.5),
        "ffn_w_down": nrm(ks[21], (DEPTH, D_FF, D_MODEL), D_FF ** -0.5),
        "final_norm_g": 1.0 + nrm(ks[22], (D_MODEL,), 0.05),
    }


def reference(x, c, ctx, c_ctx, ada_w, ada_b, norm_mix_g, norm_ffn_g, attn_w_qkv, attn_w_o,
              attn_sinks, ssm_w_in, ssm_conv_w, ssm_conv_b, ssm_dt_bias, ssm_A_log, ssm_D,
              ssm_norm_g, ssm_w_out, ffn_w_gate, ffn_w_up, ffn_w_down, final_norm_g):
    L = x.shape[1]
    cos, sin = axial_rope_tables(L)
    for i in range(DEPTH):
        last = i == DEPTH - 1
        j = i // N_MIXERS
        mod = jax.nn.silu(c) @ ada_w[i] + ada_b[i]
        sh1, sc1, g1, sh2, sc2, g2 = jnp.split(mod[:, None, :], 6, axis=-1)
        mod_c = jax.nn.silu(c_ctx) @ ada_w[i] + ada_b[i]
        csh1, csc1, cg1, csh2, csc2, cg2 = jnp.split(mod_c, 6)
        h = modulate(x, norm_mix_g[i], sh1, sc1)
        hc = modulate(ctx, norm_mix_g[i], csh1, csc1)
        if i % N_MIXERS == 0:
            y, yc = attention_mixer(h, hc, attn_w_qkv[j], attn_w_o[j], attn_sinks[j],
                                    cos, sin, not last)
        else:
            y, yc = ssm_mixer(h, hc, ssm_w_in[j], ssm_conv_w[j], ssm_conv_b[j], ssm_dt_bias[j],
                              ssm_A_log[j], ssm_D[j], ssm_norm_g[j], ssm_w_out[j], not last)
        x = x + g1 * y
        x = x + g2 * swiglu(modulate(x, norm_ffn_g[i], sh2, sc2),
                            ffn_w_gate[i], ffn_w_up[i], ffn_w_down[i])
        if not last:
            ctx = ctx + cg1 * yc
            ctx = ctx + cg2 * swiglu(modulate(ctx, norm_ffn_g[i], csh2, csc2),
                                     ffn_w_gate[i], ffn_w_up[i], ffn_w_down[i])
    return rmsnorm(x, final_norm_g)
```

```python
import numpy as np
import concourse.bass as bass
import concourse.mybir as mybir

F32 = mybir.dt.float32
BF16 = mybir.dt.bfloat16
AF = mybir.ActivationFunctionType
ALU = mybir.AluOpType
AX = mybir.AxisListType

PE, ACT, DVE, POOL, SP = "pe", "act", "dve", "pool", "sp"
ENGS = [PE, ACT, DVE, POOL, SP]
NSLOT = 3


class Tile:
    __slots__ = ("h", "n", "w", "r", "name")

    def __init__(self, h, n=1, name=""):
        self.h = h
        self.n = n
        self.w = [None] * n
        self.r = [dict() for _ in range(n)]
        self.name = name

    def __getitem__(self, k):
        return self.h[k]


class Op:
    __slots__ = ("eng", "fn", "deps", "dma", "sig", "tok", "idx")

    def __init__(self, eng, fn, dma):
        self.eng = eng
        self.fn = fn
        self.deps = set()
        self.dma = dma
        self.sig = False
        self.tok = None


def _acc(a):
    if isinstance(a, Tile):
        return a, 0, a.n
    if len(a) == 2:
        return a[0], a[1], a[1] + 1
    return a


class Prog:
    def __init__(self, nc, sem_es=None):
        self.nc = nc
        self.ops = []
        self.sem_es = sem_es
        self.esem = None
        self.cnt = {e: 0 for e in ENGS}
        self.dcnt = {e: 0 for e in (SP, POOL, ACT)}

    def op(self, eng, fn, reads=(), writes=(), dma=False):
        o = Op(eng, fn, dma)
        oid = len(self.ops)
        for a in reads:
            t, lo, hi = _acc(a)
            for i in range(lo, hi):
                if t.w[i] is not None:
                    o.deps.add(t.w[i])
        for a in writes:
            t, lo, hi = _acc(a)
            for i in range(lo, hi):
                if t.w[i] is not None:
                    o.deps.add(t.w[i])
                for rid in t.r[i].values():
                    o.deps.add(rid)
        for a in reads:
            t, lo, hi = _acc(a)
            key = (eng, oid) if dma else (eng, -1)
            for i in range(lo, hi):
                t.r[i][key] = oid
        for a in writes:
            t, lo, hi = _acc(a)
            for i in range(lo, hi):
                t.w[i] = oid
                t.r[i] = dict()
        o.deps.discard(oid)
        self.ops.append(o)
        return oid

    def mm(self, out, lhsT, rhs, start=True, stop=True, reads=(), writes=(), **kw):
        return self.op(PE, lambda e: e.matmul(out, lhsT, rhs, start=start, stop=stop, **kw), reads, writes)

    def tr(self, out, in_, ident, reads=(), writes=()):
        return self.op(PE, lambda e: e.transpose(out, in_, ident), reads, writes)

    def act(self, out, in_, func, reads=(), writes=(), **kw):
        return self.op(ACT, lambda e: e.activation(out, in_, func, **kw), reads, writes)

    def dma(self, eng, out, in_, reads=(), writes=(), **kw):
        return self.op(eng, lambda e: e.dma_start(out=out, in_=in_, **kw), reads, writes, dma=True)

    def emit(self, final_wait_ops=(), end_barrier=False):
        nc = self.nc
        ops = self.ops
        if end_barrier:
            seen = set()
            for o in reversed(ops):
                if not o.dma and o.eng not in seen:
                    o.sig = True
                    seen.add(o.eng)
        for o in ops:
            if o.eng == PE and not o.dma:
                o.deps = {d for d in o.deps if not (ops[d].eng == PE and not ops[d].dma)}
        for o in ops:
            for d in o.deps:
                ops[d].sig = True
        for d in final_wait_ops:
            ops[d].sig = True
        from contextlib import ExitStack
        with ExitStack() as es:
            ses = self.sem_es if self.sem_es is not None else es
            if self.esem is None:
                self.esem = {e: ses.enter_context(nc.semaphore("s_" + e)) for e in ENGS}
                self.dsem = {e: [ses.enter_context(nc.semaphore("d_%s%d" % (e, i))) for i in range(NSLOT)]
                             for e in (SP, POOL, ACT)}
                self.csem = ses.enter_context(nc.semaphore("s_cc"))
                self.ccnt = 0
            esem, dsem, cnt, dcnt = self.esem, self.dsem, self.cnt, self.dcnt
            per = {e: [] for e in ENGS}
            for o in ops:
                if o.dma == 'cc':
                    self.ccnt += 1
                    o.idx = -1
                    o.tok = (self.csem, self.ccnt)
                elif o.dma:
                    i = dcnt[o.eng]
                    dcnt[o.eng] += 1
                    o.idx = i
                    o.tok = (dsem[o.eng][i % NSLOT], 16 * (i // NSLOT + 1))
                elif o.sig:
                    cnt[o.eng] += 1
                    o.tok = (esem[o.eng], cnt[o.eng])
                per[o.eng].append(o)
            block = es.enter_context(nc.Block())

            def run(engname, e):
                waited = {}
                for o in per[engname]:
                    need = {}
                    for d in o.deps:
                        s, v = ops[d].tok
                        if need.get(s, (None, 0))[1] < v:
                            need[s] = (s, v)
                    if o.dma and o.dma != 'cc' and o.idx >= NSLOT:
                        s, v = o.tok
                        pv = v - 16
                        if need.get(s, (None, 0))[1] < pv:
                            need[s] = (s, pv)
                    for s, v in need.values():
                        if waited.get(s, 0) < v:
                            e.wait_ge(s, v)
                            waited[s] = v
                    ins = o.fn(e)
                    if o.dma == 'cc':
                        ins.then_inc(o.tok[0], 1)
                    elif o.dma:
                        ins.then_inc(o.tok[0], 16)
                    elif o.sig:
                        ins.then_inc(o.tok[0], 1)
                if engname == SP:
                    need = {}
                    for d in final_wait_ops:
                        s, v = ops[d].tok
                        if need.get(s, (None, 0))[1] < v:
                            need[s] = (s, v)
                    for s, v in need.values():
                        e.wait_ge(s, v)
                if end_barrier:
                    for e2 in ENGS:
                        if cnt[e2] > 0 and waited.get(esem[e2], 0) < cnt[e2]:
                            e.wait_ge(esem[e2], cnt[e2])
                    for q in (SP, POOL, ACT):
                        for i in range(NSLOT):
                            n_i = (dcnt[q] - i + NSLOT - 1) // NSLOT if dcnt[q] > i else 0
                            if n_i > 0 and waited.get(dsem[q][i], 0) < 16 * n_i:
                                e.wait_ge(dsem[q][i], 16 * n_i)

            @block.tensor
            def _(e):
                run(PE, e)

            @block.scalar
            def _(e):
                run(ACT, e)

            @block.vector
            def _(e):
                run(DVE, e)

            @block.gpsimd
            def _(e):
                run(POOL, e)

            @block.sync
            def _(e):
                run(SP, e)
        self.ops = []


from contextlib import ExitStack

D = 1024
NB = 16
DFF = 2816
NFC = 22
EPS = 1e-6


class Ctx:
    pass


def common_setup(nc, P, es, C, nxblk, nwb=4, l1=False):
    pfx = getattr(C, "pfx", "")

    def sb(name, shape, dt, n=1):
        return Tile(es.enter_context(nc.sbuf_tensor(pfx + name, shape, dt)), n, name)
    C.sb = sb
    if getattr(C, "Xh", None) is not None:
        C.X = Tile(C.Xh, 19, "X")
    else:
        C.X = sb("X", [128, nxblk, D], F32, nxblk)
    C.IDF = sb("IDF", [128, 128], F32)
    C.IDB = sb("IDB", [128, 128], BF16)
    C.ps = [Tile(es.enter_context(nc.psum_tensor(pfx + "ps%d" % i, [128, 512], F32)), 1, "ps%d" % i) for i in range(7)]
    C.psT = Tile(es.enter_context(nc.psum_tensor(pfx + "psT", [128, 8, 128], BF16)), 1, "psT")
    C.psi = 0
    C.WB = [sb("WB%d" % i, [128, 4096], BF16) for i in range(nwb)]
    C.wbi = 0
    C.JUNK = sb("JUNK", [128, D], BF16)
    C.XN = [sb("XN%d" % i, [128, D], BF16) for i in range(1 if l1 else 2)]
    C.xni = 0
    C.ST = [sb("ST%d" % i, [128, 4], F32) for i in range(4)]
    C.sti = 0
    C.TMP = [sb("TMP%d" % i, [128, 512], F32) for i in range(2 if l1 else 4)]
    C.tmi = 0
    if l1:
        C.HT = sb("HT", [128, 8, 256], BF16)
        C.BIG = sb("BIG", [128, 4224], BF16, 10)
        C.HIDv = C.BIG.h[:, 0:NFC * 128].rearrange("p (a b) -> p a b", a=NFC)
    else:
        C.HT = sb("HT", [128, 8, 512], BF16)
        C.BIG = sb("BIG", [128, NFC * 512], BF16, 10)
        C.HIDv = C.BIG.h[:, :].rearrange("p (a b) -> p a b", a=NFC)
        C.QTv = C.BIG.h[:, 0:4096].rearrange("p (a b) -> p a b", a=8)
        C.OBv = C.BIG.h[:, 4096:8192].rearrange("p (a b) -> p a b", a=4)
        C.PTv = C.BIG.h[:, 8192:10752].rearrange("p (a b) -> p a b", a=5)
    if l1:
        C.AWv = [C.BIG.h[:, 0:4096].bitcast(F32).rearrange("p (a b) -> p a b", a=8)] * 2
        C.AWd = [C.BIG, C.BIG]
    else:
        C.AWv = [C.BIG.h[:, i * 4096:(i + 1) * 4096].bitcast(F32).rearrange("p (a b) -> p a b", a=8) for i in range(2)]
        C.AWd = [(C.BIG, 0), (C.BIG, 1, 5)]
    C.SG = [sb("SG%d" % i, [128, 512], BF16) for i in range(1 if l1 else 2)]
    C.sgi = 0


def nextps(C):
    t = C.ps[C.psi % len(C.ps)]
    C.psi += 1
    return t


def nexttmp(C):
    t = C.TMP[C.tmi % len(C.TMP)]
    C.tmi += 1
    return t


def load_w(P, C, src_ap, a, b):
    t = C.WB[C.wbi % len(C.WB)]
    C.wbi += 1
    v = t.h[:, 0:a * b].rearrange("p (a b) -> p a b", a=a)
    P.dma(POOL, v, src_ap, writes=[t])
    return t, v


def wcols(w, c0, n):
    return w.rearrange("(kc p) n -> p kc n", p=128)[:, :, c0:c0 + n]


def ada_stage(P, C, nc, adaw, adab, gmix, gffn, want_ctx_gates):
    P.dma(SP, C.GMIX[:], gmix, writes=[C.GMIX])
    P.dma(SP, C.GFFN[:], gffn, writes=[C.GFFN])
    BW = 256
    for nb in range(6144 // BW):
        aw, awd = C.AWv[nb % 2], C.AWd[nb % 2]
        ab = C.ADABB[nb % 2]
        P.dma(SP, aw, wcols(adaw, nb * BW, BW), writes=[awd])
        for r in range(2):
            P.dma(SP, ab[r:r + 1, :], adab[:, nb * BW:(nb + 1) * BW], writes=[ab])
        ps = nextps(C)
        for kc in range(8):
            P.mm(ps[0:2, 0:BW], C.SIL[:, kc, :], aw[:, kc, :], start=(kc == 0), stop=(kc == 7),
                 reads=[C.SIL, awd], writes=[ps])
        mb = C.MODB[nb % 2]
        P.op(DVE, lambda e, mb=mb, ps=ps, ab=ab: e.tensor_tensor(mb[0:2, :], ps[0:2, 0:BW], ab[0:2, :], ALU.add),
             [ps, ab], [mb])
        pt = nextps(C)
        for i in range(2):
            P.tr(pt[:, 2 * i:2 * i + 2], mb[0:2, i * 128:(i + 1) * 128], C.IDF[0:2, 0:2], reads=[mb, C.IDF], writes=[pt])
        P.op(DVE, lambda e, pt=pt, nb=nb: e.tensor_copy(C.MODT[:, nb * 2:(nb + 1) * 2, :],
                                                          pt[:, 0:4].rearrange("p (a b) -> p a b", b=2)),
             [pt], [C.MODT])
        if nb in (8, 9, 10, 11, 20, 21, 22, 23):
            k = 0 if nb < 12 else 1
            q = nb % 4
            for r in range(2 if want_ctx_gates else 1):
                pg = nextps(C)
                P.mm(pg[:, 0:BW], C.SEL[0:2, r, :], mb[0:2, :], reads=[C.SEL, mb], writes=[pg])
                P.act(C.G[r][k][:, q * BW:(q + 1) * BW], pg[:, 0:BW], AF.Copy, reads=[pg], writes=[C.G[r][k]])
    for (dst, j0, g) in ((C.SC1, 8, C.GMIX), (C.SC2, 32, C.GFFN)):
        for r in range(2):
            P.op(DVE, lambda e, dst=dst, j0=j0, g=g, r=r: e.scalar_tensor_tensor(
                dst[:, :, r], C.MODT[:, j0:j0 + 8, r], 1.0, g[:], ALU.add, ALU.mult), [C.MODT, g], [dst])
    for (dst, j0) in ((C.SH1, 0), (C.SH2, 24)):
        P.op(DVE, lambda e, dst=dst, j0=j0: e.tensor_copy(dst[:], C.MODT[:, j0:j0 + 8, :]), [C.MODT], [dst])


def alloc_ada(C, nc, ng=2):
    sb = C.sb
    C.ADABB = [sb("ADABB%d" % i, [2, 256], F32) for i in range(2)]
    C.GMIX = sb("GMIX", [128, 8], F32)
    C.GFFN = sb("GFFN", [128, 8], F32)
    C.MODB = [sb("MODB%d" % i, [2, 256], F32) for i in range(2)]
    C.MODT = sb("MODT", [128, 48, 2], F32)
    C.SEL = sb("SEL", [2, 2, 128], F32)
    C.SIL = sb("SIL", [128, 8, 2], F32)
    C.CC = sb("CC", [128, 8, 2], F32)
    C.G = [[sb("G%d%d" % (r, k), [128, D], F32) for k in range(2)] for r in range(ng)]
    C.SC1 = sb("SC1", [128, 8, 2], F32)
    C.SH1 = sb("SH1", [128, 8, 2], F32)
    C.SC2 = sb("SC2", [128, 8, 2], F32)
    C.SH2 = sb("SH2", [128, 8, 2], F32)


def norm_blocks(P, C, blks, r, SC, SH, HT):
    for i, blk in enumerate(blks):
        st = C.ST[C.sti % 4]
        C.sti += 1
        xn = C.XN[C.xni % len(C.XN)]
        C.xni += 1
        P.op(POOL, lambda e, st=st: e.memset(st[:], 0.0), [], [st])
        P.act(C.JUNK[:], C.X[:, blk, :], AF.Square, reads=[(C.X, blk)], writes=[C.JUNK, st], accum_out=st[:, 0:1])
        P.act(st[:, 1:2], st[:, 0:1], AF.Ln, reads=[st], writes=[st], scale=1.0 / D, bias=EPS)
        P.act(st[:, 2:3], st[:, 1:2], AF.Exp, reads=[st], writes=[st], scale=-0.5)
        P.op(DVE, lambda e, xn=xn, blk=blk, st=st: e.tensor_scalar(xn[:], C.X[:, blk, :], st[:, 2:3], None, ALU.mult),
             [(C.X, blk), st], [xn])
        for j in range(8):
            P.tr(C.psT[:, j, :], xn[:, j * 128:(j + 1) * 128], C.IDB[:], reads=[xn, C.IDB], writes=[C.psT])
        for j in range(8):
            dst = HT[:, j, i * 128:(i + 1) * 128]
            if j % 2 == 0:
                P.act(dst, C.psT[:, j, :], AF.Identity, reads=[C.psT, SC, SH], writes=[HT],
                      scale=SC[:, j, r:r + 1], bias=SH[:, j, r:r + 1])
            else:
                P.op(DVE, lambda e, dst=dst, j=j: e.tensor_scalar(dst, C.psT[:, j, :], SC[:, j, r:r + 1], SH[:, j, r:r + 1],
                                                                    ALU.mult, ALU.add), [C.psT, SC, SH], [HT])


def resid_add(P, C, ps, blk, half, G):
    t = nexttmp(C)
    sl = slice(half * 512, (half + 1) * 512)
    P.op(DVE, lambda e: e.tensor_tensor(t[:], ps[:], G[:, sl], ALU.mult), [ps, G], [t])
    P.op(POOL, lambda e: e.tensor_tensor(C.X[:, blk, sl], C.X[:, blk, sl], t[:], ALU.add), [t, (C.X, blk)], [(C.X, blk)])


def ffn_tb(P, C, blks, r, wg, wu, wd, G2):
    nt = len(blks) * 128
    norm_blocks(P, C, blks, r, C.SC2, C.SH2, C.HT)
    for pc in range(6):
        ncx = 4 if pc < 5 else 2
        wgt, wgv = load_w(P, C, wcols(wg, pc * 512, ncx * 128), 8, ncx * 128)
        wut, wuv = load_w(P, C, wcols(wu, pc * 512, ncx * 128), 8, ncx * 128)
        for cc_ in range(ncx):
            c = pc * 4 + cc_
            pg = nextps(C)
            pu = nextps(C)
            for kc in range(8):
                P.mm(pg[:, 0:nt], wgv[:, kc, cc_ * 128:(cc_ + 1) * 128], C.HT[:, kc, 0:nt], start=(kc == 0), stop=(kc == 7),
                     reads=[wgt, C.HT], writes=[pg])
            for kc in range(8):
                P.mm(pu[:, 0:nt], wuv[:, kc, cc_ * 128:(cc_ + 1) * 128], C.HT[:, kc, 0:nt], start=(kc == 0), stop=(kc == 7),
                     reads=[wut, C.HT], writes=[pu])
            sg = C.SG[C.sgi % len(C.SG)]
            C.sgi += 1
            P.act(sg[:, 0:nt], pg[:, 0:nt], AF.Silu, reads=[pg], writes=[sg])
            P.op(DVE, lambda e, c=c, sg=sg, pu=pu: e.tensor_tensor(C.HIDv[:, c, 0:nt], sg[:, 0:nt], pu[:, 0:nt], ALU.mult),
                 [sg, pu], [C.BIG])
    for half in range(2):
        pss = [nextps(C) for _ in blks]
        for pc in range(6):
            ncx = 4 if pc < 5 else 2
            src = wd.rearrange("(kc p) n -> p kc n", p=128)[:, pc * 4:pc * 4 + ncx, half * 512:(half + 1) * 512]
            wt, wv = load_w(P, C, src, ncx, 512)
            for i, blk in enumerate(blks):
                for cc_ in range(ncx):
                    c = pc * 4 + cc_
                    P.mm(pss[i][:], C.HIDv[:, c, i * 128:(i + 1) * 128], wv[:, cc_, :], start=(c == 0), stop=(c == NFC - 1),
                         reads=[C.BIG, wt], writes=[pss[i]], skip_group_check=True)
        for i, blk in enumerate(blks):
            resid_add(P, C, pss[i], blk, half, G2)


def attn_qblock(P, C, qcol, ob_i, keys, nheads_scale=0.125):
    nk = len(keys)
    for g in range(4):
        m, e = g // 2, g % 2
        pt = C.PTv
        for s, (kb, mask) in enumerate(keys):
            ps = nextps(C)
            P.mm(ps[:].rearrange("p (a b) -> p a b", a=4),
                 C.KT[e * 64:(e + 1) * 64, m, kb * 128:(kb + 1) * 128],
                 C.QTv[e * 64:(e + 1) * 64, 4 * m:4 * m + 4, qcol:qcol + 128],
                 reads=[(C.KT, kb), (C.BIG, 0)], writes=[ps])
            P.act(pt[:, s, :], ps[:], AF.Exp, reads=[ps], writes=[(C.BIG, 5 + s)], scale=nheads_scale)
            if mask is not None:
                P.op(POOL, lambda e_, pt=pt, s=s, mask=mask: e_.tensor_tensor(pt[:, s, :], pt[:, s, :], mask[:], ALU.mult),
                     [(C.BIG, 5 + s), mask], [(C.BIG, 5 + s)])
        po = nextps(C)
        for r in range(4):
            for s, (kb, mask) in enumerate(keys):
                P.mm(po[:, r * 65:(r + 1) * 65], pt[:, s, r * 128:(r + 1) * 128], C.VA[:, kb, g, :],
                     start=(s == 0), stop=(s == nk - 1), reads=[(C.BIG, 5 + s), (C.VA, kb)], writes=[po], skip_group_check=True)
        st = C.ST[C.sti % 4]
        C.sti += 1
        pov = po[:, 0:260].rearrange("p (r c) -> p r c", c=65)
        P.op(DVE, lambda e_, st=st, pov=pov, g=g: e_.tensor_tensor(st[:, 0:4], pov[:, :, 64], C.ESINK[:, g * 4:(g + 1) * 4], ALU.add),
             [po, C.ESINK], [st])
        P.op(DVE, lambda e_, st=st: e_.reciprocal(st[:, 0:4], st[:, 0:4]), [st], [st])
        for r in range(4):
            h = g * 4 + r
            dst = C.OBv[:, ob_i, h * 64:(h + 1) * 64]
            if r % 2 == 0:
                P.act(dst, po[:, r * 65:r * 65 + 64], AF.Copy, reads=[po, st], writes=[(C.BIG, 1 + ob_i)], scale=st[:, r:r + 1])
            else:
                P.op(DVE, lambda e_, dst=dst, po=po, r=r, st=st: e_.tensor_scalar(dst, po[:, r * 65:r * 65 + 64], st[:, r:r + 1], None, ALU.mult),
                     [po, st], [(C.BIG, 1 + ob_i)])


def wo_tb(P, C, blks, wo, G1):
    for i, blk in enumerate(blks):
        for j in range(8):
            P.tr(C.psT[:, j, :], C.OBv[:, i, j * 128:(j + 1) * 128], C.IDB[:], reads=[(C.BIG, 1 + i), C.IDB], writes=[C.psT])
        dst = C.HT[:, :, i * 128:(i + 1) * 128]
        if i % 2 == 0:
            P.act(dst, C.psT[:], AF.Copy, reads=[C.psT], writes=[C.HT])
        else:
            P.op(DVE, lambda e_, dst=dst: e_.tensor_copy(dst, C.psT[:]), [C.psT], [C.HT])
    for half in range(2):
        wt, wv = load_w(P, C, wcols(wo, half * 512, 512), 8, 512)
        for i, blk in enumerate(blks):
            ps = nextps(C)
            for kc in range(8):
                P.mm(ps[:], C.HT[:, kc, i * 128:(i + 1) * 128], wv[:, kc, :], start=(kc == 0), stop=(kc == 7),
                     reads=[C.HT, wt], writes=[ps])
            resid_add(P, C, ps, blk, half, G1)


def rope_evac(P, C, ps1, ps2, cos, sin, dst, dst_acc, nt):
    t1 = nexttmp(C)
    t2 = nexttmp(C)
    P.op(DVE, lambda e_: e_.tensor_tensor(t1[:, 0:nt], ps1[:, 0:nt], cos[:, 0:nt], ALU.mult), [ps1, cos], [t1])
    P.act(t2[:, 0:nt], ps2[:, 0:nt], AF.Copy, reads=[ps2], writes=[t2])
    P.op(POOL, lambda e_: e_.tensor_tensor(t2[:, 0:nt], t2[:, 0:nt], sin[:, 0:nt], ALU.mult), [t2, sin], [t2])
    P.op(POOL, lambda e_: e_.tensor_tensor(dst, t1[:, 0:nt], t2[:, 0:nt], ALU.add), [t1, t2], dst_acc)


def proj_fm(P, C, wt, wv, col, nt):
    ps = nextps(C)
    for kc in range(8):
        P.mm(ps[:, 0:nt], wv[:, kc, col:col + 128], C.HT[:, kc, 0:nt], start=(kc == 0), stop=(kc == 7),
             reads=[wt, C.HT], writes=[ps])
    return ps


def build_l0(stop=None, F=None):
    nc = F.nc if F is not None else bass.Bass("TRN2", target_bir_lowering=False)
    pf = "a_" if F is not None else ""
    dr = lambda name, shape: nc.dram_tensor(pf + name, shape, F32, kind="ExternalInput").ap()
    xin = dr("xin", [17 * 128, D])
    ctxin = dr("ctxin", [256, D])
    cc = dr("cc", [128, 8, 2])
    adaw = dr("adaw", [D, 6144])
    adab = dr("adab", [1, 6144])
    gmix = dr("gmix", [128, 8])
    gffn = dr("gffn", [128, 8])
    wqkv = dr("wqkv", [D, 4096])
    wo = dr("wo", [D, D])
    sinks = dr("sinks", [1, 16])
    wg = dr("wg", [D, DFF])
    wu = dr("wu", [D, DFF])
    wd = dr("wd", [DFF, D])
    costab = dr("costab", [128, 17 * 128])
    sintab = dr("sintab", [128, 17 * 128])
    ident = dr("ident", [128, 128])
    maskp = dr("maskp", [128, 512])
    maskn = dr("maskn", [128, 512])
    sel = dr("sel", [2, 2, 128])
    if F is None:
        xout = nc.dram_tensor("xout", [NB * 128, D], F32, kind="ExternalOutput").ap()
        ctxout = nc.dram_tensor("ctxout", [256, D], F32, kind="ExternalOutput").ap()
    P = F.P if F is not None else Prog(nc)
    C = Ctx()
    if F is not None:
        C.pfx = "a_"
        C.Xh = F.Xh
    with ExitStack() as es:
        common_setup(nc, P, es, C, 19)
        alloc_ada(C, nc)
        sb = C.sb
        C.KT = sb("KT", [128, 2, 19 * 128], BF16, 19)
        C.VA = sb("VA", [128, 19, 4, 65], BF16, 19)
        C.ESINK = sb("ESINK", [128, 16], F32)
        C.MASKP = sb("MASKP", [128, 512], BF16)
        C.MASKN = sb("MASKN", [128, 512], BF16)
        C.COS = [sb("COS%d" % i, [128, 512], F32) for i in range(2)]
        C.SIN = [sb("SIN%d" % i, [128, 512], F32) for i in range(2)]
        csi = [0]

        def load_cs(tok0, nt):
            c_, s_ = C.COS[csi[0] % 2], C.SIN[csi[0] % 2]
            csi[0] += 1
            P.dma(SP, c_[:, 0:nt], costab[:, tok0:tok0 + nt], writes=[c_])
            P.dma(SP, s_[:, 0:nt], sintab[:, tok0:tok0 + nt], writes=[s_])
            return c_, s_

        P.dma(SP, C.IDF[:], ident, writes=[C.IDF])
        P.op(DVE, lambda e: e.tensor_copy(C.IDB[:], C.IDF[:]), [C.IDF], [C.IDB])
        P.dma(SP, C.SEL[:], sel, writes=[C.SEL])
        P.dma(SP, C.CC[:], cc, writes=[C.CC])
        P.act(C.SIL[:], C.CC[:], AF.Silu, reads=[C.CC], writes=[C.SIL])
        P.dma(POOL, C.MASKP[:], maskp, writes=[C.MASKP])
        P.dma(POOL, C.MASKN[:], maskn, writes=[C.MASKN])
        P.dma(SP, C.ESINK[:], sinks.to_broadcast([128, 16]), writes=[C.ESINK])
        P.act(C.ESINK[:], C.ESINK[:], AF.Exp, reads=[C.ESINK], writes=[C.ESINK])
        xv = xin.rearrange("(n p) d -> p n d", p=128)
        for t in range(4):
            P.dma(SP, C.X[:, 4 * t:4 * t + 4, :], xv[:, 4 * t:4 * t + 4, :], writes=[(C.X, 4 * t, 4 * t + 4)])
        P.dma(SP, C.X[:, 16:17, :], xv[:, 16:17, :], writes=[(C.X, 16)])
        P.dma(SP, C.X[:, 17:19, :], ctxin.rearrange("(n p) d -> p n d", p=128), writes=[(C.X, 17, 19)])
        P.op(POOL, lambda e: e.memset(C.VA[:, :, :, 64:65], 1.0), [], [C.VA])

        P.marks = {}
        P.marks['pre'] = len(P.ops)
        ada_stage(P, C, nc, adaw, adab, gmix, gffn, True)
        P.marks['ada'] = len(P.ops)

        wka, wkav = load_w(P, C, wcols(wqkv, 3072, 512), 8, 512)
        wkb, wkbv = load_w(P, C, wcols(wqkv, 3584, 512), 8, 512)
        groups = [([0, 1, 2, 3], 0), ([4, 5, 6, 7], 0), ([8, 9, 10, 11], 0), ([12, 13, 14, 15], 0), ([16], 0), ([17, 18], 1)]
        for blks, r in groups:
            nt = len(blks) * 128
            tok0 = blks[0] * 128
            norm_blocks(P, C, blks, r, C.SC1, C.SH1, C.HT)
            if r == 0:
                cos, sin = load_cs(tok0, nt)
                for m in range(2):
                    ps1 = proj_fm(P, C, wka, wkav, m * 128, nt)
                    ps2 = proj_fm(P, C, wka, wkav, 256 + m * 128, nt)
                    rope_evac(P, C, ps1, ps2, cos, sin, C.KT[:, m, tok0:tok0 + nt], [(C.KT, blks[0], blks[-1] + 1)], nt)
            else:
                for m in range(2):
                    ps1 = proj_fm(P, C, wkb, wkbv, m * 128, nt)
                    P.act(C.KT[:, m, tok0:tok0 + nt], ps1[:, 0:nt], AF.Copy, reads=[ps1], writes=[(C.KT, blks[0], blks[-1] + 1)])
            for i, blk in enumerate(blks):
                ps = nextps(C)
                for kc in range(8):
                    P.mm(ps[:, 0:256], C.HT[:, kc, i * 128:(i + 1) * 128], wkbv[:, kc, 256:512], start=(kc == 0), stop=(kc == 7),
                         reads=[C.HT, wkb], writes=[ps])
                P.act(C.VA[:, blk, :, 0:64], ps[:, 0:256].rearrange("p (g d) -> p g d", g=4), AF.Copy, reads=[ps], writes=[(C.VA, blk)])

        P.marks['stageA'] = len(P.ops)
        cb = [17, 18]
        norm_blocks(P, C, cb, 1, C.SC1, C.SH1, C.HT)
        for pc in range(2):
            wt, wv = load_w(P, C, wcols(wqkv, 2048 + pc * 512, 512), 8, 512)
            for c4 in range(4):
                ps1 = proj_fm(P, C, wt, wv, c4 * 128, 256)
                P.act(C.QTv[:, pc * 4 + c4, 0:256], ps1[:, 0:256], AF.Copy, reads=[ps1], writes=[(C.BIG, 0)])
        for i in range(2):
            attn_qblock(P, C, i * 128, i, [(17, None), (18, None)])
        P.marks['ctxattn'] = len(P.ops)
        wo_tb(P, C, cb, wo, C.G[1][0])
        P.marks['ctxwo'] = len(P.ops)
        ffn_tb(P, C, cb, 1, wg, wu, wd, C.G[1][1])
        outs = []
        if F is None:
            outs.append(P.dma(SP, ctxout.rearrange("(n p) d -> p n d", p=128), C.X[:, 17:19, :], reads=[(C.X, 17, 19)]))

        P.marks['ctxdone'] = len(P.ops)
        for t in range(4):
            blks = [4 * t + i for i in range(4)]
            norm_blocks(P, C, blks, 0, C.SC1, C.SH1, C.HT)
            cos, sin = load_cs(blks[0] * 128, 512)
            for pc in range(2):
                w1, w1v = load_w(P, C, wcols(wqkv, pc * 512, 512), 8, 512)
                w2, w2v = load_w(P, C, wcols(wqkv, 1024 + pc * 512, 512), 8, 512)
                for c4 in range(4):
                    ps1 = proj_fm(P, C, w1, w1v, c4 * 128, 512)
                    ps2 = proj_fm(P, C, w2, w2v, c4 * 128, 512)
                    rope_evac(P, C, ps1, ps2, cos, sin, C.QTv[:, pc * 4 + c4, :], [(C.BIG, 0)], 512)
            for i, n in enumerate(blks):
                keys = [(17, None), (18, None)]
                if n >= 1:
                    keys.append((n - 1, C.MASKP))
                keys.append((n, None))
                keys.append((n + 1, C.MASKN))
                attn_qblock(P, C, i * 128, i, keys)
            wo_tb(P, C, blks, wo, C.G[0][0])
            ffn_tb(P, C, blks, 0, wg, wu, wd, C.G[0][1])
            if F is None:
                outs.append(P.dma(SP, xout.rearrange("(n p) d -> p n d", p=128)[:, 4 * t:4 * t + 4, :], C.X[:, 4 * t:4 * t + 4, :],
                                  reads=[(C.X, 4 * t, 4 * t + 4)]))
        print('marks', P.marks, len(P.ops))
        if stop is not None:
            n = P.marks[stop] if isinstance(stop, str) else stop
            P.ops = P.ops[:n]
            outs = [o for o in outs if o < n]
        if F is not None:
            P.emit((), end_barrier=True)
        else:
            P.emit(outs)
    return nc


DIN = 2048
NH = 32


def norm_cols(P, C, blk, r, SC, SH, HT, dst0, tok0, ntok):
    st = C.ST[C.sti % 4]
    C.sti += 1
    xn = C.XN[C.xni % len(C.XN)]
    C.xni += 1
    P.op(POOL, lambda e: e.memset(st[:], 0.0), [], [st])
    P.act(C.JUNK[:], C.X[:, blk, :], AF.Square, reads=[(C.X, blk)], writes=[C.JUNK, st], accum_out=st[:, 0:1])
    P.act(st[:, 1:2], st[:, 0:1], AF.Ln, reads=[st], writes=[st], scale=1.0 / D, bias=EPS)
    P.act(st[:, 2:3], st[:, 1:2], AF.Exp, reads=[st], writes=[st], scale=-0.5)
    P.op(DVE, lambda e: e.tensor_scalar(xn[:], C.X[:, blk, :], st[:, 2:3], None, ALU.mult), [(C.X, blk), st], [xn])
    for j in range(8):
        P.tr(C.psT[:, j, :], xn[:, j * 128:(j + 1) * 128], C.IDB[:], reads=[xn, C.IDB], writes=[C.psT])
    for j in range(8):
        dst = HT[:, j, dst0:dst0 + ntok]
        src = C.psT[:, j, tok0:tok0 + ntok]
        if j % 2 == 0:
            P.act(dst, src, AF.Identity, reads=[C.psT, SC, SH], writes=[HT], scale=SC[:, j, r:r + 1], bias=SH[:, j, r:r + 1])
        else:
            P.op(DVE, lambda e, dst=dst, src=src, j=j: e.tensor_scalar(dst, src, SC[:, j, r:r + 1], SH[:, j, r:r + 1], ALU.mult, ALU.add),
                 [C.psT, SC, SH], [HT])


class L1:
    pass


def l1_alloc(C, phase):
    sb = C.sb
    C.XC = sb("XC", [128, 32, 128], BF16)
    C.XS = sb("XS", [128, DIN], BF16)
    C.BTOK = sb("BTOK", [128, 8, 128], BF16)
    C.XDT = [sb("XDT%d" % d, [128, DIN], BF16) for d in range(2 if phase == 'b' else 1)]
    C.XDTE = sb("XDTE", [128, DIN], BF16)
    C.S = sb("S", [128, DIN], F32)
    C.CW = sb("CW", [128, 32, 3], F32)
    C.CB = sb("CB", [128, 32], F32)
    C.DTB = sb("DTB", [128, 64], F32)
    C.ANEG = sb("ANEG", [128, 64], F32)
    C.TRI = [sb("TRI%d" % d, [128, 128], BF16) for d in range(2)]
    C.ONES = sb("ONES", [128, 128], BF16)
    C.DT = sb("DT", [128, 64], F32)
    C.DTA = sb("DTA", [128, 64], F32)
    C.DTAHb = sb("DTAHb", [128, 64], BF16)
    C.DTAH = sb("DTAH", [128, 64], F32)
    C.DTAL = sb("DTAL", [128, 64], F32)
    C.DTALb = sb("DTALb", [128, 64], BF16)
    C.CUM = sb("CUM", [128, 64], F32)
    C.TOT = sb("TOT", [128, 64], F32)
    C.ECUM = sb("ECUM", [128, 64], F32)
    C.TOEND = sb("TOEND", [128, 64], F32)
    C.CD = sb("CD", [128, 64], F32)
    C.CONVT = [sb("CONVT%d" % i, [128, 128], F32) for i in range(2)]
    if phase == 'b':
        C.SZ = sb("SZ", [128, DIN], BF16)
        C.HSAVE = sb("HSAVE", [128, 8, 1], BF16)
        C.RH = sb("RH", [128, 4, 128], BF16)
        C.RL = sb("RL", [128, 4, 128], BF16)
        C.E = sb("E", [128, 4, 128], BF16)
        C.MT = [sb("MT%d" % d, [128, 4, 128], BF16) for d in range(2)]
        C.CBM = [sb("CBM%d" % d, [128, 128], BF16) for d in range(2)]
        C.STRI = [sb("STRI%d" % d, [128, 128], BF16) for d in range(2)]
        C.MASK = [sb("MASK%d" % d, [128, 128], BF16) for d in range(2)]
        C.SBF = sb("SBF", [128, DIN], BF16)
        C.SFS = sb("SFS", [128, DIN], BF16)
        C.YACC = sb("YACC", [128, DIN], F32)
        C.YN = C.XDTE
        C.DSK = sb("DSK", [128, NH], F32)
        C.NG = sb("NG", [128, 16], F32)
        C.FG = sb("FG", [128, D], F32)
        C.YNTv = C.HT.h[:, :, :].rearrange("p a b -> p (a b)")[:, 0:2048].rearrange("p (a b) -> p a b", a=16)


def l1_front(P, C, blk, r, left, right, win, cols, phase):
    XBC = C.BIG.h[:, 0:32 * 130].rearrange("p (a b) -> p a b", a=32)
    norm_cols(P, C, blk, r, C.SC1, C.SH1, C.HT, 1, 0, 128)
    if left is not None:
        norm_cols(P, C, left, r, C.SC1, C.SH1, C.HT, 0, 127, 1)
    else:
        P.op(POOL, lambda e: e.memset(C.HT[:, :, 0:1], 0.0), [], [C.HT])
    if right == 'saved':
        P.op(DVE, lambda e: e.tensor_copy(C.HT[:, :, 129:130], C.HSAVE[:]), [C.HSAVE], [C.HT])
    elif right is not None:
        norm_cols(P, C, right, r, C.SC1, C.SH1, C.HT, 129, 0, 1)
    else:
        P.op(POOL, lambda e: e.memset(C.HT[:, :, 129:130], 0.0), [], [C.HT])
    if phase == 'b':
        P.op(DVE, lambda e: e.tensor_copy(C.HSAVE[:], C.HT[:, :, 1:2]), [C.HT], [C.HSAVE])
    nfc = 32 if phase == 'b' else 24
    for pc in range(nfc // 4):
        wt, wv = load_w(P, C, wcols(win, cols['x'] + pc * 512, 512), 8, 512)
        for c4 in range(4):
            ci = pc * 4 + c4
            ps = nextps(C)
            for kc in range(8):
                P.mm(ps[:, 0:130], wv[:, kc, c4 * 128:(c4 + 1) * 128], C.HT[:, kc, 0:130], start=(kc == 0), stop=(kc == 7),
                     reads=[wt, C.HT], writes=[ps])
            P.act(XBC[:, ci, :], ps[:, 0:130], AF.Copy, reads=[ps], writes=[C.BIG])
    if left is None:
        P.op(POOL, lambda e: e.memset(XBC[:, :, 0:1], 0.0), [], [C.BIG])
    if right is None:
        P.op(POOL, lambda e: e.memset(XBC[:, :, 129:130], 0.0), [], [C.BIG])
    for ci in range(nfc):
        t = C.CONVT[ci % 2]
        eng = DVE
        P.op(eng, lambda e, ci=ci, t=t: e.tensor_scalar(t[:], XBC[:, ci, 1:129], C.CW[:, ci, 1:2], C.CB[:, ci:ci + 1], ALU.mult, ALU.add),
             [C.BIG, C.CW, C.CB], [t])
        P.op(eng, lambda e, ci=ci, t=t: e.scalar_tensor_tensor(t[:], XBC[:, ci, 0:128], C.CW[:, ci, 0:1], t[:], ALU.mult, ALU.add),
             [C.BIG, C.CW, t], [t])
        P.op(eng, lambda e, ci=ci, t=t: e.scalar_tensor_tensor(t[:], XBC[:, ci, 2:130], C.CW[:, ci, 2:3], t[:], ALU.mult, ALU.add),
             [C.BIG, C.CW, t], [t])
        P.act(C.XC[:, ci, :], t[:], AF.Silu, reads=[t], writes=[C.XC])


def l1_dt(P, C, win, cols, ndir):
    n = 32 * ndir
    wt, wv = load_w(P, C, wcols(win, cols['dt'], n), 8, n)
    ps = nextps(C)
    for kc in range(8):
        P.mm(ps[:, 0:n], C.HT[:, kc, 1:129], wv[:, kc, :], start=(kc == 0), stop=(kc == 7), reads=[C.HT, wt], writes=[ps])
    P.op(DVE, lambda e: e.tensor_tensor(C.DT[:, 0:n], ps[:, 0:n], C.DTB[:, 0:n], ALU.add), [ps, C.DTB], [C.DT])
    P.act(C.DT[:, 0:n], C.DT[:, 0:n], AF.Exp, reads=[C.DT], writes=[C.DT])
    P.act(C.DT[:, 0:n], C.DT[:, 0:n], AF.Ln, reads=[C.DT], writes=[C.DT], bias=1.0)
    P.op(DVE, lambda e: e.tensor_tensor(C.DTA[:, 0:n], C.DT[:, 0:n], C.ANEG[:, 0:n], ALU.mult), [C.DT, C.ANEG], [C.DTA])
    P.op(DVE, lambda e: e.tensor_copy(C.DTAHb[:, 0:n], C.DTA[:, 0:n]), [C.DTA], [C.DTAHb])
    P.op(DVE, lambda e: e.tensor_copy(C.DTAH[:, 0:n], C.DTAHb[:, 0:n]), [C.DTAHb], [C.DTAH])
    P.op(DVE, lambda e: e.tensor_tensor(C.DTAL[:, 0:n], C.DTA[:, 0:n], C.DTAH[:, 0:n], ALU.subtract), [C.DTA, C.DTAH], [C.DTAL])
    P.op(DVE, lambda e: e.tensor_copy(C.DTALb[:, 0:n], C.DTAL[:, 0:n]), [C.DTAL], [C.DTALb])
    pc = nextps(C)
    for d in range(ndir):
        sl = slice(d * 32, (d + 1) * 32)
        P.mm(pc[:, sl], C.TRI[d][:], C.DTAHb[:, sl], start=True, stop=False, reads=[C.TRI[d], C.DTAHb], writes=[pc])
        P.mm(pc[:, sl], C.TRI[d][:], C.DTALb[:, sl], start=False, stop=True, reads=[C.TRI[d], C.DTALb], writes=[pc])
    P.mm(pc[:, 64:64 + n], C.ONES[:], C.DTAHb[:, 0:n], start=True, stop=False, reads=[C.ONES, C.DTAHb], writes=[pc])
    P.mm(pc[:, 64:64 + n], C.ONES[:], C.DTALb[:, 0:n], start=False, stop=True, reads=[C.ONES, C.DTALb], writes=[pc])
    P.op(DVE, lambda e: e.tensor_copy(C.CUM[:, 0:n], pc[:, 0:n]), [pc], [C.CUM])
    P.op(DVE, lambda e: e.tensor_copy(C.TOT[:, 0:n], pc[:, 64:64 + n]), [pc], [C.TOT])
    P.act(C.ECUM[:, 0:n], C.CUM[:, 0:n], AF.Exp, reads=[C.CUM], writes=[C.ECUM])
    P.act(C.CD[:, 0:n], C.TOT[:, 0:n], AF.Exp, reads=[C.TOT], writes=[C.CD])
    P.op(DVE, lambda e: e.tensor_tensor(C.TOEND[:, 0:n], C.TOT[:, 0:n], C.CUM[:, 0:n], ALU.subtract), [C.TOT, C.CUM], [C.TOEND])
    P.act(C.TOEND[:, 0:n], C.TOEND[:, 0:n], AF.Exp, reads=[C.TOEND], writes=[C.TOEND])


def l1_tokmajor(P, C):
    for grp in range(3):
        for j in range(8):
            ci = grp * 8 + j
            P.tr(C.psT[:, j, :], C.XC[:, ci, :], C.IDB[:], reads=[C.XC, C.IDB], writes=[C.psT])
        if grp < 2:
            P.act(C.XS[:, grp * 1024:(grp + 1) * 1024], C.psT[:].rearrange("p a b -> p (a b)"), AF.Copy, reads=[C.psT], writes=[C.XS])
        else:
            P.op(DVE, lambda e: e.tensor_copy(C.BTOK[:], C.psT[:]), [C.psT], [C.BTOK])


def mul_heads(P, C, out, in_, sc, col0, reads, writes):
    for h in range(NH):
        eng = DVE if h % 2 == 0 else POOL
        P.op(eng, lambda e, h=h: e.tensor_scalar(out[:, h * 64:(h + 1) * 64], in_[:, h * 64:(h + 1) * 64], sc[:, col0 + h:col0 + h + 1], None, ALU.mult),
             reads, writes)


def l1_states(P, C, d, xdt):
    mul_heads(P, C, C.XDTE, xdt, C.TOEND, d * 32, [xdt, C.TOEND], [C.XDTE])
    for g in range(8):
        ps = nextps(C)
        P.mm(ps[:, 0:256], C.BTOK[:, g, :], C.XDTE[:, g * 256:(g + 1) * 256], reads=[C.BTOK, C.XDTE], writes=[ps])
        for hh in range(4):
            h = g * 4 + hh
            sl = slice(h * 64, (h + 1) * 64)
            P.op(DVE, lambda e, sl=sl, h=h, ps=ps, hh=hh: e.scalar_tensor_tensor(C.S[:, sl], C.S[:, sl], C.CD[:, d * 32 + h:d * 32 + h + 1],
                                                                             ps[:, hh * 64:(hh + 1) * 64], ALU.mult, ALU.add),
                 [C.S, C.CD, ps], [C.S])


def l1_consts(P, C, dr, phase):
    P.dma(SP, C.IDF[:], dr['ident'], writes=[C.IDF])
    P.op(DVE, lambda e: e.tensor_copy(C.IDB[:], C.IDF[:]), [C.IDF], [C.IDB])
    P.dma(SP, C.SEL[:], dr['sel'], writes=[C.SEL])
    P.dma(SP, C.CC[:], dr['cc'], writes=[C.CC])
    P.act(C.SIL[:], C.CC[:], AF.Silu, reads=[C.CC], writes=[C.SIL])
    P.dma(SP, C.CW[:], dr['convw'], writes=[C.CW])
    P.dma(SP, C.CB[:], dr['convb'], writes=[C.CB])
    P.dma(SP, C.DTB[:], dr['dtbias'].to_broadcast([128, 64]), writes=[C.DTB])
    P.dma(SP, C.ANEG[:], dr['alog'].to_broadcast([128, 64]), writes=[C.ANEG])
    P.act(C.ANEG[:], C.ANEG[:], AF.Exp, reads=[C.ANEG], writes=[C.ANEG])
    P.op(DVE, lambda e: e.tensor_scalar(C.ANEG[:], C.ANEG[:], -1.0, None, ALU.mult), [C.ANEG], [C.ANEG])
    for d in range(2):
        P.dma(POOL, C.TRI[d][:], dr['tri'][d], writes=[C.TRI[d]])
    P.op(POOL, lambda e: e.memset(C.ONES[:], 1.0), [], [C.ONES])
    if phase == 'b':
        for d in range(2):
            P.dma(POOL, C.STRI[d][:], dr['stri'][d], writes=[C.STRI[d]])
            P.dma(POOL, C.MASK[d][:], dr['tri'][d], writes=[C.MASK[d]])
        P.dma(SP, C.DSK[:], dr['dskip'].to_broadcast([128, NH]), writes=[C.DSK])
        P.dma(SP, C.NG[:], dr['normg'], writes=[C.NG])
        P.dma(SP, C.FG[:], dr['fg'].to_broadcast([128, D]), writes=[C.FG])


def build_l1(phase, stop=None, F=None):
    nc = F.nc if F is not None else bass.Bass("TRN2", target_bir_lowering=False)
    pf = ("b_" if phase == 'a' else "c_") if F is not None else ""
    din = lambda name, shape: nc.dram_tensor(pf + name, shape, F32, kind="ExternalInput").ap()
    dr = {}
    xin = din("xin", [17 * 128, D]) if F is None else None
    dr['cc'] = din("cc", [128, 8, 2])
    adaw = din("adaw", [D, 6144])
    adab = din("adab", [1, 6144])
    gmix = din("gmix", [128, 8])
    gffn = din("gffn", [128, 8])
    dr['convw'] = din("convw", [128, 32, 3])
    dr['convb'] = din("convb", [128, 32])
    dr['dtbias'] = din("dtbias", [1, 64])
    dr['alog'] = din("alog", [1, 64])
    dr['ident'] = din("ident", [128, 128])
    dr['sel'] = din("sel", [2, 2, 128])
    dr['tri'] = din("tri", [2, 128, 128])
    if phase == 'a':
        win = din("win", [D, 3136])
        cols = dict(x=0, dt=3072)
        if F is None:
            ctxin = din("ctxin", [256, D])
            sfstart = nc.dram_tensor("sfstart", [16, 128, DIN], F32, kind="ExternalOutput").ap()
            sfinal = nc.dram_tensor("sfinal", [128, DIN], F32, kind="ExternalOutput").ap()
        else:
            sfstart, sfinal = F.sfstart, F.cc_in
    else:
        win = din("win", [D, 6208])
        cols = dict(z=0, x=2048, dt=6144)
        dr['stri'] = din("stri", [2, 128, 128])
        dr['dskip'] = din("dskip", [1, NH])
        dr['normg'] = din("normg", [128, 16])
        dr['fg'] = din("fg", [1, D])
        wout = din("wout", [DIN, D])
        wg = din("wg", [D, DFF])
        wu = din("wu", [D, DFF])
        wd = din("wd", [DFF, D])
        if F is None:
            sinit = din("sinit", [128, DIN])
            sfstart = din("sfstart", [16, 128, DIN])
        else:
            sfstart = F.sfstart
        yout = nc.dram_tensor("yout", [NB * 128, D], F32, kind="ExternalOutput").ap()
    P = F.P if F is not None else Prog(nc)
    C = Ctx()
    if F is not None:
        C.pfx = pf
        C.Xh = F.Xh
        flg = din("flg", [128, 2])
    with ExitStack() as es:
        common_setup(nc, P, es, C, 19 if phase == 'a' else 17, nwb=3, l1=True)
        alloc_ada(C, nc, ng=1)
        l1_alloc(C, phase)
        l1_consts(P, C, dr, phase)
        if F is None:
            xv = xin.rearrange("(n p) d -> p n d", p=128)
            for t in range(4):
                P.dma(SP, C.X[:, 4 * t:4 * t + 4, :], xv[:, 4 * t:4 * t + 4, :], writes=[(C.X, 4 * t, 4 * t + 4)])
            P.dma(SP, C.X[:, 16:17, :], xv[:, 16:17, :], writes=[(C.X, 16)])
            if phase == 'a':
                P.dma(SP, C.X[:, 17:19, :], ctxin.rearrange("(n p) d -> p n d", p=128), writes=[(C.X, 17, 19)])
        else:
            FLG = C.sb("FLG", [128, 2], F32)
            P.dma(SP, FLG[:], flg, writes=[FLG])
            groups = [[0, 1], [2, 3], [4, 5], [6, 7]]
            if phase == 'a':
                g0 = C.XS.h[:, :].bitcast(F32)
                g1 = C.XDTE.h[:, :].bitcast(F32)
                o1 = P.dma(SP, F.ccx_in[0:1, :], C.X[127:128, 15, :], reads=[(C.X, 15)], writes=[F.t_ccx_in])
                P.op(POOL, lambda e: e.collective_compute("AllGather", ALU.bypass, replica_groups=groups,
                                                           ins=[F.ccx_in.opt()], outs=[F.ccx_out.opt()]),
                     [F.t_ccx_in], [F.t_ccx_out], dma='cc')
                P.dma(SP, g0[0:1, :], F.ccx_out[0:1, :], reads=[F.t_ccx_out], writes=[C.XS])
                P.dma(SP, g1[0:1, :], F.ccx_out[1:2, :], reads=[F.t_ccx_out], writes=[C.XDTE])
                P.op(DVE, lambda e: e.tensor_scalar(C.X[0:1, 16, :], g0[0:1, :], FLG[0:1, 1:2], None, ALU.mult), [C.XS, FLG], [(C.X, 16)])
                P.op(DVE, lambda e: e.scalar_tensor_tensor(C.X[0:1, 16, :], g1[0:1, :], FLG[0:1, 0:1], C.X[0:1, 16, :], ALU.mult, ALU.add),
                     [C.XDTE, FLG, (C.X, 16)], [(C.X, 16)])
        ada_stage(P, C, nc, adaw, adab, gmix, gffn, False)
        outs = []
        P.marks = {'ada': len(P.ops)}
        if phase == 'a':
            P.op(POOL, lambda e: e.memset(C.S[:], 0.0), [], [C.S])
            seq = [(17, 1, None, 18, None), (18, 1, 17, None, None)] + \
                  [(c, 0, (c - 1 if c > 0 else None), c + 1, c) for c in range(16)]
            for blk, r, left, right, store in seq:
                if store is not None:
                    outs.append(P.dma(SP, sfstart[store], C.S[:], reads=[C.S]))
                l1_front(P, C, blk, r, left, right, win, cols, 'a')
                l1_dt(P, C, win, cols, 1)
                l1_tokmajor(P, C)
                mul_heads(P, C, C.XDT[0], C.XS, C.DT, 0, [C.XS, C.DT], [C.XDT[0]])
                l1_states(P, C, 0, C.XDT[0])
                P.marks['c%d' % blk] = len(P.ops)
            outs.append(P.dma(SP, sfinal, C.S[:], reads=[C.S]))
        else:
            if F is None:
                P.dma(SP, C.S[:], sinit, writes=[C.S])
            else:
                P.op(POOL, lambda e: e.collective_compute("AllGather", ALU.bypass, replica_groups=groups,
                                                           ins=[F.cc_in.opt()], outs=[F.cc_out.opt()]),
                     [F.t_cc_in], [F.t_cc_out], dma='cc')
                P.dma(SP, C.S[:], F.cc_out[0:128, :], reads=[F.t_cc_out], writes=[C.S])
                P.dma(SP, C.YACC[:], F.cc_out[128:256, :], reads=[F.t_cc_out], writes=[C.YACC])
                P.op(DVE, lambda e: e.tensor_scalar(C.S[:], C.S[:], FLG[:, 1:2], None, ALU.mult), [C.S, FLG], [C.S])
                P.op(DVE, lambda e: e.scalar_tensor_tensor(C.S[:], C.YACC[:], FLG[:, 0:1], C.S[:], ALU.mult, ALU.add),
                     [C.YACC, FLG, C.S], [C.S])
            XBC = C.BIG.h[:, 0:32 * 130].rearrange("p (a b) -> p a b", a=32)
            for c in range(15, -1, -1):
                blk = c
                P.dma(POOL, C.SFS[:], sfstart[c], writes=[C.SFS])
                P.op(DVE, lambda e: e.tensor_copy(C.SBF[:], C.S[:]), [C.S], [C.SBF])
                l1_front(P, C, blk, 0, (c - 1 if c > 0 else None), (16 if c == 15 else 'saved'), win, cols, 'b')
                for q in range(4):
                    wt, wv = load_w(P, C, wcols(win, cols['z'] + q * 512, 512), 8, 512)
                    ps = nextps(C)
                    for kc in range(8):
                        P.mm(ps[:], C.HT[:, kc, 1:129], wv[:, kc, :], start=(kc == 0), stop=(kc == 7), reads=[C.HT, wt], writes=[ps])
                    P.act(C.SZ[:, q * 512:(q + 1) * 512], ps[:], AF.Silu, reads=[ps], writes=[C.SZ])
                l1_dt(P, C, win, cols, 2)
                l1_tokmajor(P, C)
                for d in range(2):
                    mul_heads(P, C, C.XDT[d], C.XS, C.DT, d * 32, [C.XS, C.DT], [C.XDT[d]])
                sstate = [C.SFS, C.SBF]
                for g in range(8):
                    pcb = nextps(C)
                    P.mm(pcb[:, 0:128], C.XC[:, 16 + g, :], C.XC[:, 24 + g, :], reads=[C.XC], writes=[pcb])
                    for d in range(2):
                        P.op(DVE, lambda e, d=d, pcb=pcb: e.tensor_tensor(C.CBM[d][:], pcb[:, 0:128], C.MASK[d][:], ALU.mult),
                             [pcb, C.MASK[d]], [C.CBM[d]])
                    for d in range(2):
                        for hh in range(4):
                            col = d * 32 + g * 4 + hh
                            eng = DVE if hh % 2 == 0 else POOL
                            P.op(eng, lambda e, d=d, hh=hh, col=col: e.tensor_scalar(C.RH[:, hh, :], C.TRI[d][:], C.DTAH[:, col:col + 1], None, ALU.mult),
                                 [C.TRI[d], C.DTAH], [C.RH])
                            P.op(eng, lambda e, d=d, hh=hh, col=col: e.tensor_scalar(C.RL[:, hh, :], C.TRI[d][:], C.DTAL[:, col:col + 1], None, ALU.mult),
                                 [C.TRI[d], C.DTAL], [C.RL])
                        pd = nextps(C)
                        P.mm(pd[:], C.STRI[d][:], C.RH[:].rearrange("p a b -> p (a b)"), start=True, stop=False, reads=[C.STRI[d], C.RH], writes=[pd])
                        P.mm(pd[:], C.STRI[d][:], C.RL[:].rearrange("p a b -> p (a b)"), start=False, stop=True, reads=[C.STRI[d], C.RL], writes=[pd])
                        P.act(C.E[:].rearrange("p a b -> p (a b)"), pd[:], AF.Exp, reads=[pd], writes=[C.E])
                        for hh in range(4):
                            eng = DVE if hh % 2 == 0 else POOL
                            P.op(eng, lambda e, d=d, hh=hh: e.tensor_tensor(C.MT[d][:, hh, :], C.E[:, hh, :], C.CBM[d][:], ALU.mult),
                                 [C.E, C.CBM[d]], [C.MT[d]])
                    py = nextps(C)
                    for hh in range(4):
                        h = g * 4 + hh
                        for d in range(2):
                            P.mm(py[:, hh * 64:(hh + 1) * 64], C.MT[d][:, hh, :], C.XDT[d][:, h * 64:(h + 1) * 64], start=(d == 0), stop=(d == 1),
                                 reads=[C.MT[d], C.XDT[d]], writes=[py], skip_group_check=True)
                    P.act(C.YACC[:, g * 256:(g + 1) * 256], py[:, 0:256], AF.Copy, reads=[py], writes=[C.YACC])
                    for d in range(2):
                        po = nextps(C)
                        P.mm(po[:, 0:256], C.XC[:, 24 + g, :], sstate[d][:, g * 256:(g + 1) * 256], reads=[C.XC, sstate[d]], writes=[po])
                        for hh in range(4):
                            h = g * 4 + hh
                            sl = slice(h * 64, (h + 1) * 64)
                            P.op(DVE, lambda e, d=d, h=h, hh=hh, sl=sl, po=po: e.scalar_tensor_tensor(
                                C.YACC[:, sl], po[:, hh * 64:(hh + 1) * 64], C.ECUM[:, d * 32 + h:d * 32 + h + 1], C.YACC[:, sl], ALU.mult, ALU.add),
                                [po, C.ECUM, C.YACC], [C.YACC])
                l1_states(P, C, 1, C.XDT[1])
                for h in range(NH):
                    sl = slice(h * 64, (h + 1) * 64)
                    eng = DVE
                    P.op(eng, lambda e, h=h, sl=sl: e.scalar_tensor_tensor(C.YACC[:, sl], C.XS[:, sl], C.DSK[:, h:h + 1], C.YACC[:, sl], ALU.mult, ALU.add),
                         [C.XS, C.DSK, C.YACC], [C.YACC])
                P.op(DVE, lambda e: e.tensor_tensor(C.YACC[:], C.YACC[:], C.SZ[:], ALU.mult), [C.YACC, C.SZ], [C.YACC])
                st8 = C.TOEND
                P.op(POOL, lambda e: e.memset(st8[:, 0:8], 0.0), [C.TOEND], [C.TOEND])
                for g in range(8):
                    P.act(C.JUNK[:, 0:256], C.YACC[:, g * 256:(g + 1) * 256], AF.Square, reads=[C.YACC], writes=[C.JUNK, st8],
                          accum_out=st8[:, g:g + 1])
                P.act(st8[:, 8:16], st8[:, 0:8], AF.Ln, reads=[st8], writes=[st8], scale=1.0 / 256, bias=EPS)
                P.act(st8[:, 16:24], st8[:, 8:16], AF.Exp, reads=[st8], writes=[st8], scale=-0.5)
                for g in range(8):
                    eng = DVE if g % 2 == 0 else POOL
                    P.op(eng, lambda e, g=g: e.tensor_scalar(C.YN[:, g * 256:(g + 1) * 256], C.YACC[:, g * 256:(g + 1) * 256], st8[:, 16 + g:17 + g], None, ALU.mult),
                         [C.YACC, st8], [C.YN])
                for grp in range(2):
                    for j in range(8):
                        ci = grp * 8 + j
                        P.tr(C.psT[:, j, :], C.YN[:, ci * 128:(ci + 1) * 128], C.IDB[:], reads=[C.YN, C.IDB], writes=[C.psT])
                    for j in range(8):
                        ci = grp * 8 + j
                        P.act(C.YNTv[:, ci, :], C.psT[:, j, :], AF.Copy, reads=[C.psT, C.NG], writes=[C.HT], scale=C.NG[:, ci:ci + 1])
                for half in range(2):
                    ps = nextps(C)
                    for kh in range(2):
                        src = wout.rearrange("(kc p) n -> p kc n", p=128)[:, kh * 8:(kh + 1) * 8, half * 512:(half + 1) * 512]
                        wt, wv = load_w(P, C, src, 8, 512)
                        for k8 in range(8):
                            kc = kh * 8 + k8
                            P.mm(ps[:], C.YNTv[:, kc, :], wv[:, k8, :], start=(kc == 0), stop=(kc == 15), reads=[C.HT, wt], writes=[ps])
                    resid_add(P, C, ps, blk, half, C.G[0][0])
                ffn_tb(P, C, [blk], 0, wg, wu, wd, C.G[0][1])
                st = C.ST[C.sti % 4]
                C.sti += 1
                P.op(POOL, lambda e, st=st: e.memset(st[:], 0.0), [], [st])
                P.act(C.JUNK[:], C.X[:, blk, :], AF.Square, reads=[(C.X, blk)], writes=[C.JUNK, st], accum_out=st[:, 0:1])
                P.act(st[:, 1:2], st[:, 0:1], AF.Ln, reads=[st], writes=[st], scale=1.0 / D, bias=EPS)
                P.act(st[:, 2:3], st[:, 1:2], AF.Exp, reads=[st], writes=[st], scale=-0.5)
                P.op(DVE, lambda e, blk=blk, st=st: e.scalar_tensor_tensor(C.X[:, blk, :], C.X[:, blk, :], st[:, 2:3], C.FG[:], ALU.mult, ALU.mult),
                     [(C.X, blk), st, C.FG], [(C.X, blk)])
                outs.append(P.dma(SP, yout[blk * 128:(blk + 1) * 128, :], C.X[:, blk, :], reads=[(C.X, blk)]))
                P.marks['c%d' % blk] = len(P.ops)
        print('marks', len(P.ops))
        if stop is not None:
            n = P.marks[stop] if isinstance(stop, str) else stop
            P.ops = P.ops[:n]
            outs = [o for o in outs if o < n]
        if F is not None and phase == 'a':
            P.emit((), end_barrier=True)
        else:
            P.emit(outs)
    return nc


class Fused:
    pass


def build_fused():
    nc = bass.Bass("TRN2", target_bir_lowering=False)
    F = Fused()
    F.nc = nc
    with ExitStack() as ges:
        F.P = Prog(nc, sem_es=ges)
        F.Xh = ges.enter_context(nc.sbuf_tensor("Xres", [128, 19, D], F32))
        F.sfstart = nc.dram_tensor("sfstart_i", [16, 128, DIN], F32).ap()
        F.cc_in = nc.dram_tensor("cc_in", [128, DIN], F32).ap()
        F.cc_out = nc.dram_tensor("cc_out", [256, DIN], F32).ap()
        F.ccx_in = nc.dram_tensor("ccx_in", [1, D], F32).ap()
        F.ccx_out = nc.dram_tensor("ccx_out", [2, D], F32).ap()
        F.t_cc_in, F.t_cc_out, F.t_ccx_in, F.t_ccx_out = (Tile(None, 1, n) for n in ("cci", "cco", "cxi", "cxo"))
        build_l0(F=F)
        build_l1('a', F=F)
        build_l1('b', F=F)
    return nc


def true_pos(half, n):
    t = np.arange(n)
    return t if half == 0 else 4095 - t

def rope_tabs(pos):
    inv = (np.float32(10000.0) ** (-np.arange(16, dtype=np.float32) / np.float32(16))).astype(np.float32)
    row = (pos // 64).astype(np.float32)
    col = (pos % 64).astype(np.float32)
    n = len(pos)
    cos = np.zeros((128, n), np.float32)
    sin = np.zeros((128, n), np.float32)
    for p in range(128):
        j = p % 64
        ax, a, f = j // 32, (j % 32) // 16, j % 16
        ang = ((row if ax == 0 else col) * inv[f]).astype(np.float32)
        cos[p] = np.cos(ang)
        sin[p] = np.sin(ang) * (-1.0 if a == 0 else 1.0)
    return cos, sin

def qkv_ext(w):
    j = np.arange(64)
    ax, a, f = j // 32, (j % 32) // 16, j % 16
    p1 = j
    p2 = ax * 32 + (1 - a) * 16 + f
    cols = []
    for perm in (p1, p2, j):
        for ch in range(8):
            m, r = ch // 4, ch % 4
            for e in range(2):
                h = (2 * m + e) * 4 + r
                cols.append(h * 64 + perm)
    for perm in (p1, p2, j):
        for m in range(2):
            for e in range(2):
                g = 2 * m + e
                cols.append(1024 + g * 64 + perm)
    cols.append(1280 + np.arange(256))
    cols = np.concatenate(cols)
    assert cols.shape[0] == 4096
    return np.ascontiguousarray(w[:, cols])

def fm(v):
    return np.ascontiguousarray(v.reshape(8, 128).T)

def consts():
    j = np.arange(128)[:, None]
    i = np.arange(128)[None, :]
    maskp = np.tile((i <= j).astype(np.float32), (1, 4))
    maskn = np.tile((j <= i).astype(np.float32), (1, 4))
    sel = np.zeros((2, 2, 128), np.float32)
    sel[0, 0] = 1
    sel[1, 1] = 1
    return dict(ident=np.eye(128, dtype=np.float32), maskp=maskp, maskn=maskn, sel=sel)

def prep_l0(inp, core):
    b, half = core // 2, core % 2
    pos = true_pos(half, 17 * 128)
    cos, sin = rope_tabs(pos)
    ctx = inp["ctx"][b]
    if half == 1:
        ctx = ctx[::-1]
    cc = np.stack([fm(inp["c"][b]), fm(inp["c_ctx"])], axis=-1)
    d = dict(
        xin=np.ascontiguousarray(inp["x"][b][pos]), ctxin=np.ascontiguousarray(ctx), cc=np.ascontiguousarray(cc),
        adaw=inp["ada_w"][0], adab=inp["ada_b"][0][None, :], gmix=fm(inp["norm_mix_g"][0]), gffn=fm(inp["norm_ffn_g"][0]),
        wqkv=qkv_ext(inp["attn_w_qkv"][0]), wo=inp["attn_w_o"][0], sinks=inp["attn_sinks"][0][None, :],
        wg=inp["ffn_w_gate"][0], wu=inp["ffn_w_up"][0], wd=inp["ffn_w_down"][0], costab=cos, sintab=sin)
    d.update(consts())
    return {k: np.ascontiguousarray(v, dtype=np.float32) for k, v in d.items()}


def l1_consts_host():
    k = np.arange(128)[:, None]
    i = np.arange(128)[None, :]
    tri = np.stack([(k <= i), (k >= i)]).astype(np.float32)
    stri = np.stack([(k > i), (k < i)]).astype(np.float32)
    sel = np.zeros((2, 2, 128), np.float32)
    sel[0, 0] = 1
    sel[1, 1] = 1
    return dict(ident=np.eye(128, dtype=np.float32), sel=sel, tri=tri, stri=stri)


def halo_true(half):
    return 2048 if half == 0 else 2047


def prep_l1(inp, core, phase, x1, ctx1=None, sinit=None, sfstart=None):
    b, half = core // 2, core % 2
    pos = true_pos(half, 2048)
    xin = np.zeros((17 * 128, 1024), np.float32)
    if x1 is not None:
        xin[:2048] = x1[b][pos]
        xin[2048] = x1[b][halo_true(half)]
    w = inp["ssm_w_in"][0]
    dts = [w[:, 6144 + d * 32:6144 + (d + 1) * 32] for d in range(2)]
    p0, p1 = (0, 1) if half == 0 else (1, 0)
    cw = inp["ssm_conv_w"][0]
    if half == 1:
        cw = cw[::-1]
    convw = np.ascontiguousarray(cw.T.reshape(32, 128, 3).transpose(1, 0, 2))
    convb = np.ascontiguousarray(inp["ssm_conv_b"][0].reshape(32, 128).T)
    d = dict(xin=xin, cc=np.stack([fm(inp["c"][b]), fm(inp["c_ctx"])], axis=-1),
             adaw=inp["ada_w"][1], adab=inp["ada_b"][1][None, :], gmix=fm(inp["norm_mix_g"][1]), gffn=fm(inp["norm_ffn_g"][1]),
             convw=convw, convb=convb,
             dtbias=np.concatenate([inp["ssm_dt_bias"][0][p0], inp["ssm_dt_bias"][0][p1]])[None, :],
             alog=np.concatenate([inp["ssm_A_log"][0][p0], inp["ssm_A_log"][0][p1]])[None, :])
    c = l1_consts_host()
    d.update(ident=c["ident"], sel=c["sel"], tri=c["tri"])
    if phase == 'a':
        d.update(win=np.concatenate([w[:, 2048:5120], dts[p0], dts[p1]], axis=1))
        if ctx1 is not None:
            ctx = ctx1[b]
            if half == 1:
                ctx = ctx[::-1]
            d.update(ctxin=ctx)
    else:
        d.update(win=np.concatenate([w[:, 0:6144], dts[p0], dts[p1]], axis=1), stri=c["stri"],
                 dskip=inp["ssm_D"][0][None, :], normg=np.ascontiguousarray(inp["ssm_norm_g"][0].reshape(16, 128).T),
                 fg=inp["final_norm_g"][None, :], wout=inp["ssm_w_out"][0],
                 wg=inp["ffn_w_gate"][1], wu=inp["ffn_w_up"][1], wd=inp["ffn_w_down"][1],
                 )
        if sinit is not None:
            d.update(sinit=sinit, sfstart=sfstart)
    return {k: np.ascontiguousarray(v, dtype=np.float32) for k, v in d.items()}


def prep_fused(inp, core):
    half = core % 2
    d = {"a_" + k: v for k, v in prep_l0(inp, core).items()}
    for pf, ph in (("b_", 'a'), ("c_", 'b')):
        p = prep_l1(inp, core, ph, None)
        p.pop("xin")
        p["flg"] = np.tile(np.array([[1.0 - half, float(half)]], np.float32), (128, 1))
        d.update({pf + k: v for k, v in p.items()})
    return d


from concourse.bass_utils import run_bass_kernel_spmd

NCORES = 8


def kernel(**inp):
    inp = {k: np.asarray(v) for k, v in inp.items()}
    cores = list(range(NCORES))
    res = run_bass_kernel_spmd(build_fused(), [prep_fused(inp, c) for c in cores], core_ids=cores)
    out = np.zeros((4, 4096, 1024), np.float32)
    for c in cores:
        b, half = c // 2, c % 2
        out[b, true_pos(half, 2048)] = res.results[c]["yout"]
    return out
```
